# Optimizing a Trainium2 kernel written in Bass

```python
import jax, jax.numpy as jnp
from jax import lax
import numpy as np

D_MODEL = 2048
BATCH = 4
SEQ = 2048
DEPTH = 1

CHUNK = 64
MIX_WIDTH = D_MODEL
POOL_WIDTH = MIX_WIDTH // 2
POOL_WINDOWS = (2, 4, 8, 16)
N_POOL_GROUPS = len(POOL_WINDOWS)
POOL_GROUP = POOL_WIDTH // N_POOL_GROUPS
ATTN_WIDTH = MIX_WIDTH - POOL_WIDTH
HEAD_DIM = 128
N_HEADS = ATTN_WIDTH // HEAD_DIM
Q_BLOCK = 128
D_FF = 4 * D_MODEL
PLE_DIM = 256
IN_WIDTH = POOL_WIDTH + 3 * ATTN_WIDTH + N_HEADS
EPS = 1e-6

kernel_name = "hymba_pool_fox_sandwich_block"


def rms_norm(x, g):
    xf = x.astype(jnp.float32)
    var = jnp.mean(jnp.square(xf), axis=-1, keepdims=True)
    return (xf * lax.rsqrt(var + EPS) * g.astype(jnp.float32)).astype(x.dtype)


def multiscale_pool(u, w_pool, pool_scale):
    B, S, _ = u.shape
    ug = u.reshape(B, S, N_POOL_GROUPS, POOL_GROUP)
    pos = jnp.arange(S, dtype=jnp.float32)
    outs = []
    for gi, w in enumerate(POOL_WINDOWS):
        x_g = ug[:, :, gi].astype(jnp.float32)
        cs = jnp.cumsum(x_g, axis=1)
        lag = jnp.concatenate([jnp.zeros((B, w, POOL_GROUP), jnp.float32), cs[:, :S - w]], axis=1)
        count = jnp.minimum(pos + 1.0, float(w))[None, :, None]
        outs.append((cs - lag) / count - x_g)
    y = jnp.stack(outs, axis=2).astype(u.dtype)
    y = jnp.einsum('bsgc,gcd->bsgd', y, w_pool)
    return y.reshape(B, S, POOL_WIDTH) * pool_scale


def forgetting_attention(q, k, v, log_f):
    S = q.shape[2]
    F = jnp.cumsum(log_f, axis=-1)
    scale = HEAD_DIM ** -0.5
    outs = []
    for start in range(0, S, Q_BLOCK):
        end = start + Q_BLOCK
        qb = q[:, :, start:end]
        kb = k[:, :, :end]
        vb = v[:, :, :end]
        s = jnp.einsum('bhqd,bhkd->bhqk', qb, kb).astype(jnp.float32) * scale
        s = s + F[:, :, start:end, None] - F[:, :, None, :end]
        qpos = start + jnp.arange(Q_BLOCK)
        kpos = jnp.arange(end)
        s = jnp.where(kpos[None, :] <= qpos[:, None], s, -jnp.inf)
        pr = jax.nn.softmax(s, axis=-1).astype(v.dtype)
        outs.append(jnp.einsum('bhqk,bhkd->bhqd', pr, vb))
    return jnp.concatenate(outs, axis=2)


def setup_inputs(seed: int = 0) -> dict:
    key = jax.random.key(seed)
    ks = jax.random.split(key, 24)
    f32 = jnp.float32

    def nrm(k, shape, fan_in):
        return jax.random.normal(k, shape, f32) * (fan_in ** -0.5)

    def gain(k, shape):
        return 1.0 + 0.05 * jax.random.normal(k, shape, f32)

    return {
        "x": jax.random.normal(ks[0], (BATCH, SEQ, D_MODEL), f32),
        "p": jax.random.normal(ks[1], (DEPTH, BATCH, SEQ, PLE_DIM), f32),
        "g_pre_mix": gain(ks[2], (DEPTH, D_MODEL)),
        "w_in": nrm(ks[3], (DEPTH, D_MODEL, IN_WIDTH), D_MODEL),
        "b_f": 3.0 + 0.5 * jax.random.normal(ks[4], (DEPTH, N_HEADS), f32),
        "w_pool": nrm(ks[5], (DEPTH, N_POOL_GROUPS, POOL_GROUP, POOL_GROUP), POOL_GROUP),
        "pool_scale": gain(ks[6], (DEPTH, POOL_WIDTH)),
        "g_pool_out": gain(ks[7], (DEPTH, POOL_WIDTH)),
        "g_attn_out": gain(ks[8], (DEPTH, ATTN_WIDTH)),
        "w_out": nrm(ks[9], (DEPTH, MIX_WIDTH, D_MODEL), MIX_WIDTH),
        "g_post_mix": gain(ks[10], (DEPTH, D_MODEL)),
        "g_pre_mlp": gain(ks[11], (DEPTH, D_MODEL)),
        "w_up": nrm(ks[12], (DEPTH, D_MODEL, D_FF), D_MODEL),
        "w_down": nrm(ks[13], (DEPTH, D_FF, D_MODEL), D_FF),
        "g_post_mlp": gain(ks[14], (DEPTH, D_MODEL)),
        "g_ple_gate": gain(ks[15], (DEPTH, D_MODEL)),
        "w_gate": nrm(ks[16], (DEPTH, D_MODEL, D_MODEL), D_MODEL),
        "w_ple": nrm(ks[17], (DEPTH, PLE_DIM, D_MODEL), PLE_DIM),
        "g_post_ple": gain(ks[18], (DEPTH, D_MODEL)),
    }


def reference(x, p, g_pre_mix, w_in, b_f, w_pool, pool_scale, g_pool_out, g_attn_out,
              w_out, g_post_mix, g_pre_mlp, w_up, w_down, g_post_mlp, g_ple_gate,
              w_gate, w_ple, g_post_ple):
    B, S, _ = x.shape
    h = x
    for i in range(DEPTH):
        hn = rms_norm(h, g_pre_mix[i])
        z = hn @ w_in[i]
        o = 0
        u = z[..., o:o + POOL_WIDTH]; o += POOL_WIDTH
        q = z[..., o:o + ATTN_WIDTH]; o += ATTN_WIDTH
        k = z[..., o:o + ATTN_WIDTH]; o += ATTN_WIDTH
        v = z[..., o:o + ATTN_WIDTH]; o += ATTN_WIDTH
        f_logit = z[..., o:o + N_HEADS] + b_f[i]

        y_pool = multiscale_pool(u, w_pool[i], pool_scale[i])

        to_heads = lambda t: t.reshape(B, S, N_HEADS, HEAD_DIM).transpose(0, 2, 1, 3)
        log_f = jax.nn.log_sigmoid(f_logit.astype(jnp.float32)).transpose(0, 2, 1)
        y_attn = forgetting_attention(to_heads(q), to_heads(k), to_heads(v), log_f)
        y_attn = y_attn.transpose(0, 2, 1, 3).reshape(B, S, ATTN_WIDTH)

        y = jnp.concatenate([rms_norm(y_pool, g_pool_out[i]),
                             rms_norm(y_attn, g_attn_out[i])], axis=-1)
        h = h + rms_norm(y @ w_out[i], g_post_mix[i])

        hn = rms_norm(h, g_pre_mlp[i])
        m = jnp.square(jax.nn.relu(hn @ w_up[i])) @ w_down[i]
        h = h + rms_norm(m, g_post_mlp[i])

        gate = jax.nn.sigmoid(rms_norm(h, g_ple_gate[i]) @ w_gate[i])
        e = p[i].astype(h.dtype) @ w_ple[i]
        h = h + rms_norm(gate * e, g_post_ple[i])
    return h
```

```python
from contextlib import ExitStack
import os
import numpy as np
import concourse.bass as bass
import concourse.mybir as mybir
from concourse.bass_utils import run_bass_kernel_spmd

F32 = mybir.dt.float32
BF16 = mybir.dt.bfloat16
AF = mybir.ActivationFunctionType
ALU = mybir.AluOpType
AX = mybir.AxisListType

D = 2048
S = 2048
B = 4
NC = 8
TOK = 1024
NT = 8
KC = 16
DFF = 8192
PLE = 256
NH = 8
HD = 128
INW = 4104
EPS = 1e-6
SCALE = HD ** -0.5
WINDOWS = (2, 4, 8, 16)


class Ev:
    __slots__ = ("sem", "key", "val")

    def __init__(self, sem, key, val):
        self.sem, self.key, self.val = sem, key, val


class Eng:
    def __init__(self, fw, name, sem, handle):
        self.fw, self.name, self.sem, self.h = fw, name, sem, handle
        self.count = 0
        self.waited = {}

    def wait(self, ev):
        if ev is None:
            return
        if isinstance(ev, (list, tuple)):
            for e in ev:
                self.wait(e)
            return
        if self.waited.get(ev.key, 0) >= ev.val:
            return
        self.waited[ev.key] = ev.val
        self.h.wait_ge(ev.sem, ev.val)

    def op(self, fn, deps=(), signal=True):
        self.wait(deps)
        ins = fn(self.h)
        if signal:
            self.count += 1
            ins.then_inc(self.sem, 1)
            return Ev(self.sem, self.name, self.count)
        return None

    def dma(self, out, in_, deps=(), slot=0):
        self.wait(deps)
        s = self.fw.slots[slot]
        s[1] += 16
        self.h.dma_start(out=out, in_=in_).then_inc(s[0], 16)
        return Ev(s[0], s[2], s[1])


class FW:
    def __init__(self, nc, n_dma_slots):
        self.nc = nc
        self.es = ExitStack()
        self.engs = {}
        handles = {"pe": nc.tensor, "act": nc.scalar, "dve": nc.vector, "pool": nc.gpsimd, "sp": nc.sync}
        for name in ("pe", "act", "dve", "pool", "sp"):
            sem = self.es.enter_context(nc.semaphore("s_" + name))
            self.engs[name] = Eng(self, name, sem, handles[name])
        self.pe, self.act, self.dve, self.pool, self.sp = (
            self.engs[n] for n in ("pe", "act", "dve", "pool", "sp"))
        self.slots = []
        for i in range(n_dma_slots):
            sem = self.es.enter_context(nc.semaphore("s_dma%d" % i))
            self.slots.append([sem, 0, "dma%d" % i])

    def sbuf(self, name, shape, dt):
        return self.es.enter_context(self.nc.sbuf_tensor("sb_" + name, shape, dt))

    def psum(self, name, shape, dt):
        return self.es.enter_context(self.nc.psum_tensor("ps_" + name, shape, dt))

    def finish(self):
        self.es.close()


SL_W0, SL_W1, SL_X0, SL_X1, SL_G, SL_MISC, SL_OUT, SL_P, SL_PL0, SL_PL1 = range(10)


def build_nc(debug=None):
    nc = bass.Bass("TRN2", target_bir_lowering=False)

    def din(name, shape):
        return nc.dram_tensor(name, list(shape), F32, kind="ExternalInput").ap()

    xo = din("xo", (TOK, D))
    xp = din("xp", (TOK, D))
    p_in = din("p", (TOK, PLE))
    w_in = din("w_in", (D, INW))
    w_out = din("w_out", (D, D))
    w_up = din("w_up", (D, DFF))
    w_down = din("w_down", (DFF, D))
    w_gate = din("w_gate", (D, D))
    w_ple = din("w_ple", (PLE, D))
    w_pool = din("w_pool", (4, 256, 256))
    gc_in = din("gc", (128, 4 * KC))
    gb_in = din("gb", (3, 128, D))
    psb_in = din("psb", (128, 1024))
    bf_in = din("bfb", (128, NH))
    poolA_in = din("poolA", (128, 12 * 128))
    kval_in = din("kval", (128, 16))
    out = nc.dram_tensor("out", [TOK, D], F32, kind="ExternalOutput").ap()
    dbg = None
    if debug:
        dbg = nc.dram_tensor("dbg", list(debug[1]), F32, kind="ExternalOutput").ap()

    fw = FW(nc, 10)
    pe, act, dve, pool, sp = fw.pe, fw.act, fw.dve, fw.pool, fw.sp

    X = fw.sbuf("X", [128, 32768], BF16)
    H = fw.sbuf("H", [128, 32768], BF16)
    hnX = fw.sbuf("hnX", [128, 8192], BF16)
    Wb = [fw.sbuf("Wb%d" % i, [128, 8192], BF16) for i in range(2)]
    GA = fw.sbuf("GA", [128, 2048], F32)
    xsb = fw.sbuf("xs", [128, 2, 2048], BF16)
    tmpf = fw.sbuf("tmpf", [128, 2, 512], F32)
    ptb = fw.sbuf("ptb", [128, 3, 128], BF16)
    fst = fw.sbuf("fst", [128, 4, 128], F32)
    ident = fw.sbuf("ident", [128, 128], BF16)
    trif = fw.sbuf("trif", [128, 128], F32)
    onesf = fw.sbuf("onesf", [128, 128], F32)
    maskb = fw.sbuf("maskb", [128, 128], BF16)
    onesb = fw.sbuf("onesb", [128, 128], BF16)
    PA = fw.sbuf("PA", [128, 2048], BF16)
    gc = fw.sbuf("gc", [128, 4, KC], F32)
    bfb = fw.sbuf("bfb", [128, NH], F32)
    kval = fw.sbuf("kval", [128, 16], F32)
    stats = fw.sbuf("stats", [128, 512], F32)
    pb = fw.psum("pb", [128, 8, 512], F32)

    hnT_prev = X[:, 0:16384].rearrange("p (k t) -> p k t", k=KC)
    hnT_own = X[:, 16384:32768].rearrange("p (k t) -> p k t", k=KC)
    yT = [X[:, 0:8192].rearrange("p (k t) -> p k t", k=KC),
          X[:, 8192:16384].rearrange("p (k t) -> p k t", k=KC)]
    m_lo = [X[:, 0:8192].bitcast(F32).rearrange("p (t n) -> p t n", t=4),
            X[:, 8192:16384].bitcast(F32).rearrange("p (t n) -> p t n", t=4)]
    h1 = X[:, 16384:32768].bitcast(F32).rearrange("p (t n) -> p t n", t=4)
    hidden = H[:, :].rearrange("p (k t) -> p k t", k=64)
    yb = H[:, 0:16384].rearrange("p (t n) -> p t n", t=NT)
    xt = [H[:, 0:4096].bitcast(F32), H[:, 4096:8192].bitcast(F32)]
    ge = H[:, 0:16384].bitcast(F32).rearrange("p (t n) -> p t n", t=4)
    QT = H[:, 16384:18432].rearrange("p (h t) -> p h t", h=2)
    KT = H[:, 18432:22528].rearrange("p (h t) -> p h t", h=2)
    Vaug = H[:, 22528:26688].rearrange("p (t h d) -> p t h d", t=16, h=2)
    u_bf = H[:, 16384:25600].rearrange("p (t n) -> p t n", t=9)
    psb = H[:, 28672:30720].bitcast(F32)
    pin = H[:, 16384:18432].bitcast(F32).rearrange("p (t n) -> p t n", t=4)
    pbf = H[:, 18432:19456].rearrange("p (t n) -> p t n", t=4)
    pT = H[:, 19456:20480].rearrange("p (k t) -> p k t", k=2)
    poolA = PA[:, 0:1536].rearrange("p (a n) -> p a n", a=12)
    wplb = PA[:, :].rearrange("p (b k n) -> p b k n", b=2, k=2)
    pooledT = hnX[:, :].rearrange("p (k t) -> p k t", k=8)
    hn2T = hnX[:, :].rearrange("p (k t) -> p k t", k=KC)
    m_hi = hnX[:, :].bitcast(F32).rearrange("p (t n) -> p t n", t=4)
    biasT = GA[:, 0:1024].rearrange("p (h i j) -> p h i j", h=NH, i=NT)
    wps = GA[:, 1024:2048].bitcast(BF16).rearrange("p (g c n) -> p g c n", g=4, c=2)
    gbuf = GA
    zf, Gt, offt, tott = fst[:, 0, :], fst[:, 1, :], fst[:, 2, :], fst[:, 3, :]
    tpv = pb[:, 6:8, :].rearrange("p a n -> p (a n)").bitcast(BF16).rearrange(
        "p (k t) -> p k t", k=KC)

    def bank(b):
        return pb[:, b, :]

    bank_free = [None] * 8

    wstate = {"n": 0, "free": [None, None]}

    def wload(src_ap, nk, ncol):
        i = wstate["n"] % 2
        wstate["n"] += 1
        view = Wb[i][:, 0:nk * ncol].rearrange("p (k n) -> p k n", k=nk)
        deps = [wstate["free"][i]]
        nsplit = 4 if nk >= 4 else 1
        step = nk // nsplit
        ev = None
        for s_ in range(nsplit):
            ev = pool.dma(view[:, s_ * step:(s_ + 1) * step, :], src_ap[:, s_ * step:(s_ + 1) * step, :],
                          deps=deps, slot=SL_W0 + i)
            deps = []
        return view, ev, i

    def wrelease(i, ev):
        wstate["free"][i] = ev

    def wview(w_ap, r0, nrows, c0, ncol):
        return w_ap[r0:r0 + nrows, c0:c0 + ncol].rearrange("(k p) n -> p k n", p=128)

    e_gc = sp.dma(gc[:].rearrange("p a k -> p (a k)"), gc_in, slot=SL_MISC)
    e_bf = sp.dma(bfb[:], bf_in, slot=SL_MISC)
    e_kv = sp.dma(kval[:], kval_in, slot=SL_MISC)
    e_psb = sp.dma(psb, psb_in, slot=SL_MISC)
    e_misc = e_psb
    e_pa = pool.dma(PA[:, 0:1536], poolA_in, slot=SL_G)
    e_wp = pool.dma(wps, w_pool.rearrange("g (c p) n -> p g c n", p=128), slot=SL_G)
    e_const = e_wp

    e_st0 = dve.op(lambda h: h.memset(stats[:], 0.0), deps=[e_misc])
    e_o = dve.op(lambda h: h.memset(onesf[:], 1.0))
    e_t0 = dve.op(lambda h: h.memset(trif[:], 1.0))
    e_fs = dve.op(lambda h: h.memset(fst[:], 0.0))
    e_tri = pool.op(lambda h: h.affine_select(trif[:], trif[:], [[1, 128]], ALU.is_ge, 0.0,
                                              base=0, channel_multiplier=-1), deps=[e_t0])
    e_idf = pool.op(lambda h: h.affine_select(tmpf[:, 0, 0:128], onesf[:], [[1, 128]], ALU.is_equal, 0.0,
                                              base=0, channel_multiplier=-1), deps=[e_o])
    e_id = dve.op(lambda h: h.tensor_copy(ident[:], tmpf[:, 0, 0:128]), deps=[e_idf])
    e_mk = dve.op(lambda h: h.tensor_copy(maskb[:], trif[:]), deps=[e_tri])
    e_ob = dve.op(lambda h: h.memset(onesb[:], 1.0))
    e_wps = dve.op(lambda h: h.tensor_tensor(
        wps, wps, psb.rearrange("p (g n) -> p g n", g=4).unsqueeze(2).to_broadcast([128, 4, 2, 256]),
        ALU.mult), deps=[e_const, e_misc])

    STAT = {"n": 0}

    def stat_cols(n):
        c = STAT["n"]
        STAT["n"] += n
        assert STAT["n"] <= 512
        return c

    def rstd_from(cols, n, width, deps):
        c = stat_cols(2)
        if n > 1:
            e = dve.op(lambda h: h.tensor_reduce(stats[:, c:c + 1], stats[:, cols:cols + n], AX.X, ALU.add),
                       deps=deps)
            src = stats[:, c:c + 1]
        else:
            e = None
            src = stats[:, cols:cols + 1]
        d2 = [e] if e is not None else list(deps)
        e1 = act.op(lambda h: h.activation(stats[:, c + 1:c + 2], src, AF.Sqrt, bias=EPS, scale=1.0 / width),
                    deps=d2)
        e2 = dve.op(lambda h: h.reciprocal(stats[:, c + 1:c + 2], stats[:, c + 1:c + 2]), deps=[e1])
        return stats[:, c + 1:c + 2], e2

    tpv2 = pb[:, 4:6, :].rearrange("p a n -> p (a n)").bitcast(BF16).rearrange("p (k t) -> p k t", k=KC)
    tpvs = [tpv, tpv2]
    TPB = [(6, 7), (4, 5)]
    tp_state = {"n": 0}

    def transpose_to(src_bf, gi, dstT, t0, deps):
        k = tp_state["n"] % 2
        tp_state["n"] += 1
        b0, b1 = TPB[k]
        tv = tpvs[k]
        d = list(deps) + [bank_free[b0], bank_free[b1], e_id]
        e = None
        for kc in range(KC):
            e = pe.op(lambda h, kc=kc: h.transpose(tv[:, kc, :], src_bf[:, kc * 128:(kc + 1) * 128], ident[:]),
                      deps=d, signal=(kc == KC - 1))
            d = []
        ev = dve.op(lambda h: h.tensor_tensor(
            dstT[:, :, t0:t0 + 128], tv, gc[:, gi, :].unsqueeze(2).to_broadcast([128, KC, 128]), ALU.mult),
            deps=[e, e_misc])
        bank_free[b0] = ev
        bank_free[b1] = ev
        return ev

    xs_free = [None, None]
    xs_n = {"n": 0}

    def junk(width):
        k = xs_n["n"] % 2
        return xsb[:, k, 0:width], xs_free[k]

    def norm_T(src, gi, dstT, t0, deps):
        k = xs_n["n"] % 2
        xs_n["n"] += 1
        xs = xsb[:, k, :]
        c = stat_cols(1)
        e_sq = act.op(lambda h: h.activation(xs, src, AF.Square, accum_out=stats[:, c:c + 1]),
                      deps=list(deps) + [xs_free[k], e_st0])
        r, e_r = rstd_from(c, 1, float(D), [e_sq])
        e_xs = act.op(lambda h: h.activation(xs, src, AF.Copy, scale=r), deps=[e_r])
        ev = transpose_to(xs, gi, dstT, t0, [e_xs])
        xs_free[k] = ev
        return ev, e_xs

    def dbg_exit(name, view, dep):
        if debug and debug[0] == name:
            e_d = sp.dma(dbg, view, deps=[dep], slot=SL_OUT)
            sp.wait(e_d)
            fw.finish()
            return True
        return False

    def hnT_all(kc, tile):
        if tile < 8:
            return hnT_prev[:, kc, tile * 128:(tile + 1) * 128]
        return hnT_own[:, kc, (tile - 8) * 128:(tile - 7) * 128]

    rot = {"n": 0}

    def next_bank4():
        b_ = rot["n"] % 4
        rot["n"] += 1
        return b_

    wu = []
    for blk in range(2):
        wu.append(wload(wview(w_in, 0, D, blk * 512, 512), KC, 512))

    def pooled_unit(cc, half):
        g = cc // 2
        b_ = next_bank4()
        d = [e_u, e_const, bank_free[b_]]
        e = None
        for i4 in range(4):
            i = half * 4 + i4
            kd = 0 if i == 0 else 1
            pe.op(lambda h: h.matmul(
                bank(b_)[:, i4 * 128:(i4 + 1) * 128], u_bf[:, i + 1, cc * 128:(cc + 1) * 128],
                poolA[:, g * 3 + kd, :], start=True, stop=False), deps=d, signal=False)
            d = []
            e = pe.op(lambda h: h.matmul(
                bank(b_)[:, i4 * 128:(i4 + 1) * 128], u_bf[:, i, cc * 128:(cc + 1) * 128],
                poolA[:, g * 3 + 2, :], start=False, stop=True), signal=(i4 == 3))
        ev_ = act.op(lambda h: h.activation(
            pooledT[:, cc, half * 512:(half + 1) * 512], bank(b_), AF.Copy), deps=[e])
        bank_free[b_] = ev_
        return ev_

    units = [(cc, half) for cc in range(8) for half in range(2)]
    xt_free = [None, None]
    e_hn = {}
    e_u = None
    e_pT = None
    order = [7] + list(range(8, 16)) + list(range(0, 7))
    for idx, ti in enumerate(order):
        src_d = xp if ti < 8 else xo
        r0 = (ti % 8) * 128
        b_ = idx % 2
        e_ld = sp.dma(xt[b_], src_d[r0:r0 + 128, :], deps=[xt_free[b_]], slot=SL_X0 + b_)
        dstT = hnT_prev if ti < 8 else hnT_own
        ev, e_xs = norm_T(xt[b_], 0, dstT, r0, [e_ld])
        xt_free[b_] = e_xs
        e_hn[ti] = ev
        if ti >= 7:
            t9 = ti - 7
            for blk in range(2):
                wv_, e_wu, wi = wu[blk]
                bk = next_bank4()
                d = [e_wu, ev, bank_free[bk]]
                for kc in range(KC):
                    e = pe.op(lambda h, kc=kc: h.matmul(bank(bk), hnT_all(kc, ti), wv_[:, kc, :],
                                                        start=(kc == 0), stop=(kc == KC - 1)),
                              deps=d, signal=(kc == KC - 1))
                    d = []
                e_u = act.op(lambda h: h.activation(
                    u_bf[:, t9, blk * 512:(blk + 1) * 512], bank(bk), AF.Copy), deps=[e, e_psb, e_wps])
                bank_free[bk] = e_u
                if ti == 15:
                    wrelease(wi, e)
        else:
            for _ in range(2):
                if units:
                    e_pT = pooled_unit(*units.pop(0))
    while units:
        e_pT = pooled_unit(*units.pop(0))
    e_hnT = e_hn[6]
    e_hn_own = e_hn[15]

    if dbg_exit("hnT", X[:, 16384:32768].bitcast(F32), e_hnT):
        return nc

    ssq_pool = stat_cols(NT)
    yp = pb[:, 4:6, :].rearrange("p a n -> p (a n)")
    e_yp = None
    for i in range(NT):
        d = [e_pT, e_wps, bank_free[4], bank_free[5]]
        for g in range(4):
            for c2 in range(2):
                e = pe.op(lambda h, i=i, g=g, c2=c2: h.matmul(
                    yp[:, g * 256:(g + 1) * 256], pooledT[:, 2 * g + c2, i * 128:(i + 1) * 128], wps[:, g, c2, :],
                    start=(c2 == 0), stop=(c2 == 1)), deps=d, signal=(g == 3 and c2 == 1))
                d = []
        e1 = act.op(lambda h, i=i: h.activation(yb[:, i, 0:1024], yp, AF.Copy), deps=[e])
        jk, jd = junk(1024)
        e_yp = act.op(lambda h, i=i, jk=jk: h.activation(jk, yp, AF.Square,
                                                         accum_out=stats[:, ssq_pool + i:ssq_pool + i + 1]),
                      deps=[jd])
        bank_free[4] = e_yp
        bank_free[5] = e_yp


    wf, e_wf, wfi = wload(wview(w_in, 0, D, 4096, 8), KC, 8)
    d = [e_wf, e_hnT, bank_free[0]]
    for t in range(16):
        for kc in range(KC):
            e = pe.op(lambda h, t=t, kc=kc: h.matmul(bank(0)[:, t * 8:(t + 1) * 8], hnT_all(kc, t), wf[:, kc, :],
                                                     start=(kc == 0), stop=(kc == KC - 1)),
                      deps=d, signal=(t == 15 and kc == KC - 1))
            d = []
    wrelease(wfi, e)
    e_z = dve.op(lambda h: h.tensor_tensor(
        zf.rearrange("p (t h) -> p t h", t=16), bank(0)[:, 0:128].rearrange("p (t h) -> p t h", t=16),
        bfb[:].unsqueeze(1).to_broadcast([128, 16, NH]), ALU.add), deps=[e, e_misc, e_fs])
    e_e = act.op(lambda h: h.activation(zf, zf, AF.Exp, scale=-1.0), deps=[e_z])
    e_l = act.op(lambda h: h.activation(zf, zf, AF.Ln, bias=1.0), deps=[e_e])
    a3_, a3_dep = junk(384)
    a3_k = xs_n["n"] % 2
    a3 = a3_.rearrange("p (k n) -> p k n", k=3)
    rres = tmpf[:, 1, 0:128]
    e_a = dve.op(lambda h: h.tensor_copy(a3[:, 0, :], zf), deps=[e_l, a3_dep])
    e_a = dve.op(lambda h: h.tensor_tensor(rres, zf, a3[:, 0, :], ALU.subtract), deps=[e_a])
    e_a = dve.op(lambda h: h.tensor_copy(a3[:, 1, :], rres), deps=[e_a])
    e_a = dve.op(lambda h: h.tensor_tensor(rres, rres, a3[:, 1, :], ALU.subtract), deps=[e_a])
    e_a = dve.op(lambda h: h.tensor_copy(a3[:, 2, :], rres), deps=[e_a])
    d = [e_a, e_mk, e_ob, bank_free[1]]
    for k3 in range(3):
        e_w = pe.op(lambda h, k3=k3: h.matmul(bank(1)[:, 0:128], maskb[:], a3[:, k3, :], start=(k3 == 0), stop=(k3 == 2)),
                    deps=d)
        d = []
    for k3 in range(3):
        e_tt = pe.op(lambda h, k3=k3: h.matmul(bank(1)[:, 128:256], onesb[:], a3[:, k3, :], start=(k3 == 0), stop=(k3 == 2)))
    xs_free[a3_k] = e_tt
    e_tot = dve.op(lambda h: h.tensor_copy(tott, bank(1)[:, 128:256]), deps=[e_tt])
    e_off = e_fs
    tot3 = tott.rearrange("p (t h) -> p t h", t=16)
    off3 = offt.rearrange("p (t h) -> p t h", t=16)
    G3 = Gt.rearrange("p (t h) -> p t h", t=16)
    for t in range(1, 16):
        e_off = dve.op(lambda h, t=t: h.tensor_tensor(off3[:, t, :], off3[:, t - 1, :], tot3[:, t - 1, :], ALU.add),
                       deps=[e_off, e_tot])
    e_G = dve.op(lambda h: h.tensor_tensor(Gt, bank(1)[:, 0:128], offt, ALU.add), deps=[e_w, e_off])
    bank_free[0] = e_z
    bank_free[1] = e_G
    e_bias = None
    for hh in range(NH):
        for i in range(NT):
            e_bias = dve.op(lambda h, hh=hh, i=i: h.tensor_scalar(
                biasT[:, hh, i, :], G3[:, :, hh], off3[:, 8 + i, hh:hh + 1], None, ALU.subtract),
                deps=[e_G])

    if dbg_exit("G", GA[:, 0:1024], e_bias):
        return nc
    if dbg_exit("G2", fst[:].rearrange("p a n -> p (a n)"), e_bias):
        return nc

    if dbg_exit("yb", H[:, 0:16384].bitcast(F32), e_yp):
        return nc
    if debug and debug[0] == "poolall":
        e1_ = sp.dma(dbg[:, 0:1024], GA[:, 1024:2048], deps=[e_yp], slot=SL_OUT)
        e2_ = sp.dma(dbg[:, 1024:5120], hnX[:, :].bitcast(F32), deps=[e_yp], slot=SL_OUT)
        e3_ = sp.dma(dbg[:, 5120:8192], H[:, 16384:22528].bitcast(F32), deps=[e_yp], slot=SL_OUT)
        sp.wait(e3_)
        fw.finish()
        return nc

    ssq_attn = stat_cols(NT * NH)
    e_vone = dve.op(lambda h: h.tensor_copy(
        Vaug[:, :, :, 128:129], kval[:].unsqueeze(2).unsqueeze(3).to_broadcast([128, 16, 2, 1])),
        deps=[e_misc, e_pT, e_yp])
    srot = {"n": 0}
    s_free = [None] * 3
    pt_free = [None] * 3
    o_free = [None, None]
    S_BANKS = (2, 3, 4)
    O_BANKS = (5, 6)
    prot2 = {"n": 0}

    def next_bank2():
        b_ = prot2["n"] % 2
        prot2["n"] += 1
        return b_

    e_att_last = None
    for hp in range(4):
        wq, e_wq, wqi = wload(wview(w_in, 0, D, 1024 + hp * 256, 256), KC, 256)
        for hh in range(2):
            for tb in range(2):
                b_ = next_bank2()
                d = [e_wq, e_hn_own, bank_free[b_]]
                for kc in range(KC):
                    e = pe.op(lambda h, kc=kc, hh=hh, tb=tb, b_=b_: h.matmul(
                        bank(b_), wq[:, kc, hh * 128:(hh + 1) * 128], hnT_own[:, kc, tb * 512:(tb + 1) * 512],
                        start=(kc == 0), stop=(kc == KC - 1)), deps=d, signal=(kc == KC - 1))
                    d = []
                ee = act.op(lambda h, hh=hh, tb=tb, b_=b_: h.activation(
                    QT[:, hh, tb * 512:(tb + 1) * 512], bank(b_), AF.Copy), deps=[e, e_att_last, e_yp])
                bank_free[b_] = ee
        wrelease(wqi, e)
        e_q = ee
        wk, e_wk, wki = wload(wview(w_in, 0, D, 2048 + hp * 256, 256), KC, 256)
        for hh in range(2):
            for tb in range(4):
                b_ = next_bank2()
                d = [e_wk, e_hnT, bank_free[b_]]
                src = hnT_prev if tb < 2 else hnT_own
                for kc in range(KC):
                    e = pe.op(lambda h, kc=kc, hh=hh, tb=tb, b_=b_, src=src: h.matmul(
                        bank(b_), wk[:, kc, hh * 128:(hh + 1) * 128], src[:, kc, (tb % 2) * 512:(tb % 2 + 1) * 512],
                        start=(kc == 0), stop=(kc == KC - 1)), deps=d, signal=(kc == KC - 1))
                    d = []
                ee = dve.op(lambda h, hh=hh, tb=tb, b_=b_: h.tensor_copy(
                    KT[:, hh, tb * 512:(tb + 1) * 512], bank(b_)), deps=[e, e_att_last, e_yp])
                bank_free[b_] = ee
        wrelease(wki, e)
        e_k = ee
        wv, e_wv, wvi = wload(wview(w_in, 0, D, 3072 + hp * 256, 256), KC, 256)
        for tile in range(16):
            b_ = next_bank2()
            d = [e_wv, e_hnT, bank_free[b_]]
            for kc in range(KC):
                e = pe.op(lambda h, kc=kc, tile=tile, b_=b_: h.matmul(
                    bank(b_)[:, 0:256], hnT_all(kc, tile), wv[:, kc, :],
                    start=(kc == 0), stop=(kc == KC - 1)), deps=d, signal=(kc == KC - 1))
                d = []
            ee = dve.op(lambda h, tile=tile, b_=b_: h.tensor_scalar(
                Vaug[:, tile, :, 0:128], bank(b_)[:, 0:256].rearrange("p (h d) -> p h d", h=2),
                kval[:, tile:tile + 1], None, ALU.mult), deps=[e, e_att_last, e_vone, e_yp])
            bank_free[b_] = ee
        wrelease(wvi, e)
        e_v = ee
        for hh in range(2):
            hd = 2 * hp + hh
            for i in range(NT):
                nj = 8 + i + 1
                ob = (hd * NT + i) % 2
                oacc = bank(O_BANKS[ob])[:, 0:129]
                pend = []

                def emit_s(j, i=i, hh=hh, hd=hd):
                    sl = srot["n"] % 3
                    srot["n"] += 1
                    st = bank(S_BANKS[sl])[:, 0:128]
                    es = pe.op(lambda h: h.matmul(st, KT[:, hh, j * 128:(j + 1) * 128],
                                                  QT[:, hh, i * 128:(i + 1) * 128], start=True, stop=True),
                               deps=[e_q, e_k, s_free[sl], bank_free[S_BANKS[sl]]])
                    pt = ptb[:, sl, :]
                    ee_ = act.op(lambda h: h.activation(pt, st, AF.Exp, scale=SCALE, bias=biasT[:, hd, i, j:j + 1]),
                                 deps=[es, pt_free[sl], e_bias])
                    s_free[sl] = ee_
                    if j == 8 + i:
                        ee_ = dve.op(lambda h: h.tensor_tensor(pt, pt, maskb[:], ALU.mult), deps=[ee_, e_mk])
                    return (j, sl, pt, ee_)

                LA = 2
                for j in range(min(LA, nj)):
                    pend.append(emit_s(j))
                for j in range(nj):
                    if j + LA < nj:
                        pend.append(emit_s(j + LA))
                    jj, sl, pt, ee_ = pend.pop(0)
                    assert jj == j
                    epv = pe.op(lambda h, j=j, pt=pt, oacc=oacc, hh=hh, nj=nj: h.matmul(
                        oacc, pt, Vaug[:, j, hh, 0:129], start=(j == 0), stop=(j == nj - 1)),
                        deps=[ee_, e_v, o_free[ob], bank_free[O_BANKS[ob]]] if j == 0 else [ee_])
                    pt_free[sl] = epv
                c = stat_cols(1)
                e_r = dve.op(lambda h, oacc=oacc, c=c: h.reciprocal(stats[:, c:c + 1], oacc[:, 128:129]), deps=[epv])
                e_y = act.op(lambda h, oacc=oacc, c=c, i=i, hd=hd: h.activation(
                    yb[:, i, 1024 + hd * 128:1024 + (hd + 1) * 128], oacc[:, 0:128], AF.Copy, scale=stats[:, c:c + 1]),
                    deps=[e_r])
                jk, jd = junk(128)
                e_y2 = act.op(lambda h, oacc=oacc, c=c, i=i, hd=hd, jk=jk: h.activation(
                    jk, oacc[:, 0:128], AF.Square, scale=stats[:, c:c + 1],
                    accum_out=stats[:, ssq_attn + i * NH + hd:ssq_attn + i * NH + hd + 1]), deps=[jd])
                o_free[ob] = e_y2
                e_att_last = e_y2
    for b_ in (2, 3, 4, 5, 6):
        bank_free[b_] = e_att_last

    e_yT = None
    for i in range(NT):
        rp, e_rp = rstd_from(ssq_pool + i, 1, 1024.0, [e_yp])
        ra, e_ra = rstd_from(ssq_attn + i * NH, NH, 1024.0, [e_att_last])
        e1 = dve.op(lambda h, i=i, rp=rp: h.tensor_scalar(yb[:, i, 0:1024], yb[:, i, 0:1024], rp, None, ALU.mult),
                    deps=[e_rp, e_att_last])
        e2 = dve.op(lambda h, i=i, ra=ra: h.tensor_scalar(yb[:, i, 1024:2048], yb[:, i, 1024:2048], ra, None, ALU.mult),
                    deps=[e_ra])
        e_yT = transpose_to(yb[:, i, :], 3, yT[i // 4], (i % 4) * 128, [e2, bank_free[6], bank_free[7]])

    if debug and debug[0] == "yT":
        e_d = sp.dma(dbg, X[:, 0:16384].bitcast(F32), deps=[e_yT], slot=SL_OUT)
        sp.wait(e_d)
        fw.finish()
        return nc

    setrot = {"n": 0}

    def next_set():
        s_ = setrot["n"] % 2
        setrot["n"] += 1
        return [4 * s_ + k for k in range(4)]

    gb_free = e_att_last
    e_blk_prev = e_yT
    out_evs = []
    h1_free = [None] * 4
    for tb in range(2):
        yTb = yT[tb]
        tok0 = tb * 512
        e_gb = sp.dma(gbuf[:], gb_in[0], deps=[gb_free], slot=SL_G)
        ssq1 = stat_cols(16)
        e_ev = [None] * 4
        for cb in range(4):
            w, e_w_, wi = wload(wview(w_out, 0, D, cb * 512, 512), KC, 512)
            bs = next_set()
            for t in range(4):
                b_ = bs[t]
                d = [e_w_, e_yT, bank_free[b_]]
                for kc in range(KC):
                    e = pe.op(lambda h, kc=kc, t=t, b_=b_, w=w: h.matmul(
                        bank(b_), yTb[:, kc, t * 128:(t + 1) * 128], w[:, kc, :],
                        start=(kc == 0), stop=(kc == KC - 1)), deps=d, signal=(kc == KC - 1))
                    d = []
                e_c = dve.op(lambda h, t=t, cb=cb, b_=b_: h.tensor_copy(h1[:, t, cb * 512:(cb + 1) * 512], bank(b_)),
                             deps=[e, h1_free[t]])
                jk, jd = junk(512)
                e_s = act.op(lambda h, t=t, cb=cb, b_=b_, jk=jk: h.activation(
                    jk, bank(b_), AF.Square, accum_out=stats[:, ssq1 + t * 4 + cb:ssq1 + t * 4 + cb + 1]),
                    deps=[e, e_c, jd])
                bank_free[b_] = [e_c, e_s]
                e_ev[t] = [e_c, e_s]
            wrelease(wi, e)
        e_h1 = [None] * 4
        e_hn2 = None
        for t in range(4):
            b2 = t % 2
            e_ld = sp.dma(xt[b2], xo[tok0 + t * 128:tok0 + (t + 1) * 128, :], deps=[xt_free[b2], e_blk_prev],
                          slot=SL_X0 + b2)
            r, e_r = rstd_from(ssq1 + t * 4, 4, float(D), e_ev[t])
            ea = dve.op(lambda h, t=t, r=r: h.scalar_tensor_tensor(h1[:, t, :], h1[:, t, :], r, gbuf[:], ALU.mult, ALU.mult),
                        deps=[e_r, e_gb])
            eb = dve.op(lambda h, t=t, b2=b2: h.tensor_tensor(h1[:, t, :], h1[:, t, :], xt[b2], ALU.add), deps=[ea, e_ld])
            xt_free[b2] = eb
            e_h1[t] = eb
            e_hn2, _ = norm_T(h1[:, t, :], 1, hn2T, t * 128, [eb])
        gb_free = e_h1[3]

        if debug and debug[0] == "h1" and tb == 0:
            e_d = sp.dma(dbg, X[:, 16384:32768].bitcast(F32), deps=[e_hn2], slot=SL_OUT)
            sp.wait(e_d)
            fw.finish()
            return nc

        e_gb = sp.dma(gbuf[:], gb_in[1], deps=[gb_free], slot=SL_G)
        e_hid = None
        tr = 0
        tmp_free = [None, None]
        for fb in range(16):
            w, e_w_, wi = wload(wview(w_up, 0, D, fb * 512, 512), KC, 512)
            bs = next_set()
            for fc in range(4):
                b_ = bs[fc]
                d = [e_w_, e_hn2, bank_free[b_]]
                for kc in range(KC):
                    e = pe.op(lambda h, kc=kc, fc=fc, b_=b_, w=w: h.matmul(
                        bank(b_), w[:, kc, fc * 128:(fc + 1) * 128], hn2T[:, kc, :],
                        start=(kc == 0), stop=(kc == KC - 1)), deps=d, signal=(kc == KC - 1))
                    d = []
                tb2 = tr % 2
                tr += 1
                e_r = act.op(lambda h, b_=b_, tb2=tb2: h.activation(tmpf[:, tb2, :], bank(b_), AF.Relu),
                             deps=[e, tmp_free[tb2]])
                e_hid = dve.op(lambda h, fb=fb, fc=fc, tb2=tb2: h.tensor_tensor(
                    hidden[:, fb * 4 + fc, :], tmpf[:, tb2, :], tmpf[:, tb2, :], ALU.mult),
                    deps=[e_r, e_blk_prev, e_h1[3]])
                tmp_free[tb2] = e_hid
                bank_free[b_] = e_r
            wrelease(wi, e)

        ssq2 = stat_cols(16)
        e_ev = [[] for _ in range(4)]
        e_m = None
        for q in range(4):
            bs = next_set()
            for kg in range(4):
                w, e_w_, wi = wload(wview(w_down, kg * 2048, 2048, q * 512, 512), KC, 512)
                for t in range(4):
                    b_ = bs[t]
                    d = [e_w_, e_hid] + ([bank_free[b_]] if kg == 0 else [])
                    for kc in range(KC):
                        e = pe.op(lambda h, kc=kc, kg=kg, t=t, b_=b_, w=w: h.matmul(
                            bank(b_), hidden[:, kg * 16 + kc, t * 128:(t + 1) * 128], w[:, kc, :],
                            start=(kg == 0 and kc == 0), stop=(kg == 3 and kc == KC - 1)),
                            deps=d, signal=(kc == KC - 1))
                        d = []
                    if kg == 3:
                        mdst = (m_lo[tb] if q < 2 else m_hi)[:, t, (q % 2) * 512:(q % 2 + 1) * 512]
                        e_c = dve.op(lambda h, mdst=mdst, b_=b_: h.tensor_copy(mdst, bank(b_)),
                                     deps=[e, e_hn2])
                        jk, jd = junk(512)
                        e_s = act.op(lambda h, t=t, q=q, b_=b_, jk=jk: h.activation(
                            jk, bank(b_), AF.Square,
                            accum_out=stats[:, ssq2 + t * 4 + q:ssq2 + t * 4 + q + 1]),
                            deps=[e, e_c, jd])
                        bank_free[b_] = [e_c, e_s]
                        e_ev[t] += [e_c, e_s]
                        e_m = e_c
                wrelease(wi, e)
        e_wdown_done = e
        e_h2 = [None] * 4
        for t in range(4):
            r, e_r = rstd_from(ssq2 + t * 4, 4, float(D), e_ev[t])
            ea = dve.op(lambda h, t=t, r=r: h.scalar_tensor_tensor(
                m_lo[tb][:, t, :], m_lo[tb][:, t, :], r, gbuf[:, 0:1024], ALU.mult, ALU.mult), deps=[e_r, e_gb])
            eb = dve.op(lambda h, t=t, r=r: h.scalar_tensor_tensor(
                m_hi[:, t, :], m_hi[:, t, :], r, gbuf[:, 1024:2048], ALU.mult, ALU.mult), deps=[e_r])
            ec = dve.op(lambda h, t=t: h.tensor_tensor(h1[:, t, 0:1024], h1[:, t, 0:1024], m_lo[tb][:, t, :], ALU.add),
                        deps=[ea])
            ed = dve.op(lambda h, t=t: h.tensor_tensor(h1[:, t, 1024:2048], h1[:, t, 1024:2048], m_hi[:, t, :], ALU.add),
                        deps=[eb])
            e_h2[t] = ed
        gb_free = e_h2[3]
        e_gb = sp.dma(gbuf[:], gb_in[2], deps=[gb_free], slot=SL_G)
        e_hn3 = None
        for t in range(4):
            e_hn3, _ = norm_T(h1[:, t, :], 2, hn2T, t * 128, [e_h2[3]])

        e_pl = sp.dma(pin, p_in[tok0:tok0 + 512, :].rearrange("(t p) n -> p t n", p=128),
                      deps=[e_wdown_done], slot=SL_P)
        e_pb = dve.op(lambda h: h.tensor_copy(pbf, pin), deps=[e_pl])
        d = [e_pb, bank_free[6], bank_free[7], e_hn3]
        for t in range(4):
            for c2 in range(2):
                e = pe.op(lambda h, t=t, c2=c2: h.transpose(
                    tpv[:, t * 2 + c2, :], pbf[:, t, c2 * 128:(c2 + 1) * 128], ident[:]),
                    deps=d, signal=(t == 3 and c2 == 1))
                d = []
        e_pT = dve.op(lambda h: h.tensor_copy(
            pT.rearrange("p k (t n) -> p t k n", t=4), tpv[:, 0:8, :].rearrange("p (t k) n -> p t k n", t=4)),
            deps=[e])
        bank_free[6] = e_pT
        bank_free[7] = e_pT
        wpl_free = [None, None]
        ssq3 = stat_cols(16)
        e_ev = [[] for _ in range(4)]
        prot = 0
        last_e = None
        for cb in range(4):
            wpl = wplb[:, cb % 2, :, :]
            e_wpl = pool.dma(wpl, w_ple[:, cb * 512:(cb + 1) * 512].rearrange("(k p) n -> p k n", p=128),
                             deps=[wpl_free[cb % 2], e_yp], slot=SL_PL0 + cb % 2)
            w, e_w_, wi = wload(wview(w_gate, 0, D, cb * 512, 512), KC, 512)
            for t in range(4):
                bg, be = 2 * (prot % 4), 2 * (prot % 4) + 1
                prot += 1
                d = [e_w_, e_hn3, bank_free[bg], bank_free[be], e_wpl, e_pT]
                for kc in range(KC):
                    e = pe.op(lambda h, kc=kc, t=t, bg=bg, w=w: h.matmul(
                        bank(bg), hn2T[:, kc, t * 128:(t + 1) * 128], w[:, kc, :],
                        start=(kc == 0), stop=(kc == KC - 1)), deps=d, signal=False)
                    d = []
                for c2 in range(2):
                    e = pe.op(lambda h, c2=c2, t=t, be=be, wpl=wpl: h.matmul(
                        bank(be), pT[:, c2, t * 128:(t + 1) * 128], wpl[:, c2, :],
                        start=(c2 == 0), stop=(c2 == 1)), signal=(c2 == 1))
                tb2 = tr % 2
                tr += 1
                e_sg = act.op(lambda h, bg=bg, tb2=tb2: h.activation(tmpf[:, tb2, :], bank(bg), AF.Sigmoid),
                              deps=[e, tmp_free[tb2]])
                e_ge = dve.op(lambda h, t=t, cb=cb, be=be, tb2=tb2: h.tensor_tensor(
                    ge[:, t, cb * 512:(cb + 1) * 512], tmpf[:, tb2, :], bank(be), ALU.mult),
                    deps=[e_sg, e_wdown_done])
                e_q3 = dve.op(lambda h, t=t, cb=cb, tb2=tb2: h.scalar_tensor_tensor(
                    tmpf[:, tb2, :], ge[:, t, cb * 512:(cb + 1) * 512], 1.0, ge[:, t, cb * 512:(cb + 1) * 512],
                    ALU.mult, ALU.mult, accum_out=stats[:, ssq3 + t * 4 + cb:ssq3 + t * 4 + cb + 1]),
                    deps=[e_ge])
                tmp_free[tb2] = e_q3
                bank_free[bg] = e_sg
                bank_free[be] = e_ge
                e_ev[t] = [e_q3]
                last_e = e
            wrelease(wi, last_e)
            wpl_free[cb % 2] = last_e
        for t in range(4):
            r, e_r = rstd_from(ssq3 + t * 4, 4, float(D), e_ev[t])
            ea = dve.op(lambda h, t=t, r=r: h.scalar_tensor_tensor(ge[:, t, :], ge[:, t, :], r, gbuf[:], ALU.mult, ALU.mult),
                        deps=[e_r, e_gb])
            eb = dve.op(lambda h, t=t: h.tensor_tensor(ge[:, t, :], ge[:, t, :], h1[:, t, :], ALU.add), deps=[ea])
            e_o_ = sp.dma(out[tok0 + t * 128:tok0 + (t + 1) * 128, :], ge[:, t, :], deps=[eb], slot=SL_OUT)
            out_evs.append(e_o_)
            e_blk_prev = eb
        e_blk_prev = [e_blk_prev, out_evs[-1]]
        gb_free = eb

    sp.wait(out_evs[-1])
    fw.finish()
    return nc


def _pool_mats(first_half):
    A = np.zeros((128, 4, 3, 128), np.float32)
    s_ = np.arange(128)[:, None]
    t_ = np.arange(128)[None, :]
    for g, w in enumerate(WINDOWS):
        for kind in range(3):
            dist = (t_ + 128 - s_) if kind == 2 else (t_ - s_)
            inwin = (dist >= 0) & (dist < w)
            if kind == 0 and first_half:
                cnt = np.minimum(t_ + 1, w).astype(np.float32)
            else:
                cnt = np.full((1, 128), float(w), np.float32)
            M = np.where(inwin, 1.0 / cnt, 0.0).astype(np.float32)
            if kind != 2:
                M = M - np.eye(128, dtype=np.float32)
            A[:, g, kind, :] = M
    return np.ascontiguousarray(A.reshape(128, 12 * 128))


def _make_in_maps(inp):
    f = lambda a: np.ascontiguousarray(np.asarray(a, dtype=np.float32))
    x = f(inp["x"])
    p = f(inp["p"])[0]
    gcs = [f(inp["g_pre_mix"])[0], f(inp["g_pre_mlp"])[0], f(inp["g_ple_gate"])[0],
           np.concatenate([f(inp["g_pool_out"])[0], f(inp["g_attn_out"])[0]])]
    gc = np.stack([g.reshape(KC, 128).T for g in gcs], axis=1).reshape(128, 4 * KC)
    gb = np.stack([np.broadcast_to(f(inp[k])[0], (128, D)) for k in ("g_post_mix", "g_post_mlp", "g_post_ple")])
    common = {
        "w_in": f(inp["w_in"])[0], "w_out": f(inp["w_out"])[0], "w_up": f(inp["w_up"])[0],
        "w_down": f(inp["w_down"])[0], "w_gate": f(inp["w_gate"])[0], "w_ple": f(inp["w_ple"])[0],
        "w_pool": f(inp["w_pool"])[0],
        "gc": np.ascontiguousarray(gc), "gb": np.ascontiguousarray(gb),
        "psb": np.ascontiguousarray(np.broadcast_to(f(inp["pool_scale"])[0], (128, 1024))),
        "bfb": np.ascontiguousarray(np.broadcast_to(f(inp["b_f"])[0], (128, NH))),
    }
    pa = [_pool_mats(False), _pool_mats(True)]
    maps = []
    for c in range(NC):
        b, hf = c // 2, c % 2
        m = dict(common)
        m["xo"] = np.ascontiguousarray(x[b, hf * TOK:(hf + 1) * TOK])
        m["xp"] = np.ascontiguousarray(x[b, 0:TOK]) if hf == 1 else np.zeros((TOK, D), np.float32)
        m["p"] = np.ascontiguousarray(p[b, hf * TOK:(hf + 1) * TOK])
        kv = np.ones((128, 16), np.float32)
        if hf == 0:
            kv[:, 0:8] = 0.0
        m["kval"] = kv
        m["poolA"] = pa[1 if hf == 0 else 0]
        maps.append(m)
    return maps


def kernel(**inputs):
    maps = _make_in_maps(inputs)
    nc = build_nc()
    res = run_bass_kernel_spmd(nc, maps, core_ids=list(range(NC)))
    outp = np.zeros((B, S, D), np.float32)
    for c in range(NC):
        b, hf = c // 2, c % 2
        outp[b, hf * TOK:(hf + 1) * TOK] = np.asarray(res.results[c]["out"], dtype=np.float32)
    return outp
```

```python
from contextlib import ExitStack
import os
import numpy as np
import concourse.bass as bass
import concourse.mybir as mybir
from concourse.bass_utils import run_bass_kernel_spmd

F32 = mybir.dt.float32
BF16 = mybir.dt.bfloat16
AF = mybir.ActivationFunctionType
ALU = mybir.AluOpType
AX = mybir.AxisListType

D = 2048
S = 2048
B = 4
NC = 8
TOK = 1024
NT = 8
KC = 16
DFF = 8192
PLE = 256
NH = 8
HD = 128
INW = 4104
EPS = 1e-6
SCALE = HD ** -0.5
WINDOWS = (2, 4, 8, 16)


class Ev:
    __slots__ = ("sem", "key", "val")

    def __init__(self, sem, key, val):
        self.sem, self.key, self.val = sem, key, val


class Eng:
    def __init__(self, fw, name, sem, handle):
        self.fw, self.name, self.sem, self.h = fw, name, sem, handle
        self.count = 0
        self.waited = {}

    def wait(self, ev):
        if ev is None:
            return
        if isinstance(ev, (list, tuple)):
            for e in ev:
                self.wait(e)
            return
        if self.waited.get(ev.key, 0) >= ev.val:
            return
        self.waited[ev.key] = ev.val
        self.h.wait_ge(ev.sem, ev.val)

    def op(self, fn, deps=(), signal=True):
        self.wait(deps)
        ins = fn(self.h)
        if signal:
            self.count += 1
            ins.then_inc(self.sem, 1)
            return Ev(self.sem, self.name, self.count)
        return None

    def dma(self, out, in_, deps=(), slot=0):
        self.wait(deps)
        s = self.fw.slots[slot]
        s[1] += 16
        self.h.dma_start(out=out, in_=in_).then_inc(s[0], 16)
        return Ev(s[0], s[2], s[1])


class FW:
    def __init__(self, nc, n_dma_slots):
        self.nc = nc
        self.es = ExitStack()
        self.engs = {}
        handles = {"pe": nc.tensor, "act": nc.scalar, "dve": nc.vector, "pool": nc.gpsimd, "sp": nc.sync}
        for name in ("pe", "act", "dve", "pool", "sp"):
            sem = self.es.enter_context(nc.semaphore("s_" + name))
            self.engs[name] = Eng(self, name, sem, handles[name])
        self.pe, self.act, self.dve, self.pool, self.sp = (
            self.engs[n] for n in ("pe", "act", "dve", "pool", "sp"))
        self.slots = []
        for i in range(n_dma_slots):
            sem = self.es.enter_context(nc.semaphore("s_dma%d" % i))
            self.slots.append([sem, 0, "dma%d" % i])

    def sbuf(self, name, shape, dt):
        return self.es.enter_context(self.nc.sbuf_tensor("sb_" + name, shape, dt))

    def psum(self, name, shape, dt):
        return self.es.enter_context(self.nc.psum_tensor("ps_" + name, shape, dt))

    def finish(self):
        self.es.close()


SL_W0, SL_W1, SL_X0, SL_X1, SL_G, SL_MISC, SL_OUT, SL_P, SL_PL0, SL_PL1 = range(10)


def build_nc(debug=None):
    nc = bass.Bass("TRN2", target_bir_lowering=False)

    def din(name, shape):
        return nc.dram_tensor(name, list(shape), F32, kind="ExternalInput").ap()

    xo = din("xo", (TOK, D))
    xp = din("xp", (TOK, D))
    p_in = din("p", (TOK, PLE))
    w_in = din("w_in", (D, INW))
    w_out = din("w_out", (D, D))
    w_up = din("w_up", (D, DFF))
    w_down = din("w_down", (DFF, D))
    w_gate = din("w_gate", (D, D))
    w_ple = din("w_ple", (PLE, D))
    w_pool = din("w_pool", (4, 256, 256))
    gc_in = din("gc", (128, 4 * KC))
    gb_in = din("gb", (3, 128, D))
    psb_in = din("psb", (128, 1024))
    bf_in = din("bfb", (128, NH))
    poolA_in = din("poolA", (128, 12 * 128))
    kval_in = din("kval", (128, 16))
    out = nc.dram_tensor("out", [TOK, D], F32, kind="ExternalOutput").ap()
    dbg = None
    if debug:
        dbg = nc.dram_tensor("dbg", list(debug[1]), F32, kind="ExternalOutput").ap()

    fw = FW(nc, 10)
    pe, act, dve, pool, sp = fw.pe, fw.act, fw.dve, fw.pool, fw.sp

    X = fw.sbuf("X", [128, 32768], BF16)
    H = fw.sbuf("H", [128, 32768], BF16)
    hnX = fw.sbuf("hnX", [128, 8192], BF16)
    Wb = [fw.sbuf("Wb%d" % i, [128, 8192], BF16) for i in range(2)]
    GA = fw.sbuf("GA", [128, 2048], F32)
    xsb = fw.sbuf("xs", [128, 2, 2048], BF16)
    tmpf = fw.sbuf("tmpf", [128, 2, 512], F32)
    ptb = fw.sbuf("ptb", [128, 2, 256], BF16)
    fst = fw.sbuf("fst", [128, 4, 128], F32)
    ident = fw.sbuf("ident", [128, 128], BF16)
    trif = fw.sbuf("trif", [128, 128], F32)
    onesf = fw.sbuf("onesf", [128, 128], F32)
    maskb = fw.sbuf("maskb", [128, 128], BF16)
    onesb = fw.sbuf("onesb", [128, 128], BF16)
    PA = fw.sbuf("PA", [128, 2048], BF16)
    gc = fw.sbuf("gc", [128, 4, KC], F32)
    bfb = fw.sbuf("bfb", [128, NH], F32)
    kval = fw.sbuf("kval", [128, 16], F32)
    stats = fw.sbuf("stats", [128, 512], F32)
    pb = fw.psum("pb", [128, 8, 512], F32)

    hnT_prev = X[:, 0:16384].rearrange("p (k t) -> p k t", k=KC)
    hnT_own = X[:, 16384:32768].rearrange("p (k t) -> p k t", k=KC)
    yT = [X[:, 0:8192].rearrange("p (k t) -> p k t", k=KC),
          X[:, 8192:16384].rearrange("p (k t) -> p k t", k=KC)]
    m_lo = [X[:, 0:8192].bitcast(F32).rearrange("p (t n) -> p t n", t=4),
            X[:, 8192:16384].bitcast(F32).rearrange("p (t n) -> p t n", t=4)]
    h1 = X[:, 16384:32768].bitcast(F32).rearrange("p (t n) -> p t n", t=4)
    hidden = H[:, :].rearrange("p (k t) -> p k t", k=64)
    yb = H[:, 0:16384].rearrange("p (t n) -> p t n", t=NT)
    xt = [H[:, 0:4096].bitcast(F32), H[:, 4096:8192].bitcast(F32)]
    ge = H[:, 0:16384].bitcast(F32).rearrange("p (t n) -> p t n", t=4)
    QT = H[:, 16384:18432].rearrange("p (h t) -> p h t", h=2)
    KT = H[:, 18432:22528].rearrange("p (h t) -> p h t", h=2)
    Vaug = H[:, 22528:26688].rearrange("p (t h d) -> p t h d", t=16, h=2)
    u_bf = H[:, 16384:25600].rearrange("p (t n) -> p t n", t=9)
    psb = H[:, 28672:30720].bitcast(F32)
    pin = H[:, 16384:18432].bitcast(F32).rearrange("p (t n) -> p t n", t=4)
    pbf = H[:, 18432:19456].rearrange("p (t n) -> p t n", t=4)
    pT = H[:, 19456:20480].rearrange("p (k t) -> p k t", k=2)
    poolA = PA[:, 0:1536].rearrange("p (a n) -> p a n", a=12)
    wplb = PA[:, :].rearrange("p (b k n) -> p b k n", b=2, k=2)
    pooledT = hnX[:, :].rearrange("p (k t) -> p k t", k=8)
    hn2T = hnX[:, :].rearrange("p (k t) -> p k t", k=KC)
    m_hi = hnX[:, :].bitcast(F32).rearrange("p (t n) -> p t n", t=4)
    biasT = GA[:, 0:1024].rearrange("p (h i j) -> p h i j", h=NH, i=NT)
    wps = GA[:, 1024:2048].bitcast(BF16).rearrange("p (g c n) -> p g c n", g=4, c=2)
    gbuf = GA
    zf, Gt, offt, tott = fst[:, 0, :], fst[:, 1, :], fst[:, 2, :], fst[:, 3, :]
    tpv = pb[:, 6:8, :].rearrange("p a n -> p (a n)").bitcast(BF16).rearrange(
        "p (k t) -> p k t", k=KC)

    def bank(b):
        return pb[:, b, :]

    bank_free = [None] * 8

    wstate = {"n": 0, "free": [None, None]}

    def wload(src_ap, nk, ncol):
        i = wstate["n"] % 2
        wstate["n"] += 1
        view = Wb[i][:, 0:nk * ncol].rearrange("p (k n) -> p k n", k=nk)
        deps = [wstate["free"][i]]
        nsplit = 4 if nk >= 4 else 1
        step = nk // nsplit
        ev = None
        for s_ in range(nsplit):
            ev = pool.dma(view[:, s_ * step:(s_ + 1) * step, :], src_ap[:, s_ * step:(s_ + 1) * step, :],
                          deps=deps, slot=SL_W0 + i)
            deps = []
        return view, ev, i

    def wrelease(i, ev):
        wstate["free"][i] = ev

    def wview(w_ap, r0, nrows, c0, ncol):
        return w_ap[r0:r0 + nrows, c0:c0 + ncol].rearrange("(k p) n -> p k n", p=128)

    e_gc = sp.dma(gc[:].rearrange("p a k -> p (a k)"), gc_in, slot=SL_MISC)
    e_bf = sp.dma(bfb[:], bf_in, slot=SL_MISC)
    e_kv = sp.dma(kval[:], kval_in, slot=SL_MISC)
    e_psb = sp.dma(psb, psb_in, slot=SL_MISC)
    e_misc = e_psb
    e_pa = pool.dma(PA[:, 0:1536], poolA_in, slot=SL_G)
    e_wp = pool.dma(wps, w_pool.rearrange("g (c p) n -> p g c n", p=128), slot=SL_G)
    e_const = e_wp

    e_st0 = dve.op(lambda h: h.memset(stats[:], 0.0), deps=[e_misc])
    e_o = dve.op(lambda h: h.memset(onesf[:], 1.0))
    e_t0 = dve.op(lambda h: h.memset(trif[:], 1.0))
    e_fs = dve.op(lambda h: h.memset(fst[:], 0.0))
    e_tri = pool.op(lambda h: h.affine_select(trif[:], trif[:], [[1, 128]], ALU.is_ge, 0.0,
                                              base=0, channel_multiplier=-1), deps=[e_t0])
    e_idf = pool.op(lambda h: h.affine_select(tmpf[:, 0, 0:128], onesf[:], [[1, 128]], ALU.is_equal, 0.0,
                                              base=0, channel_multiplier=-1), deps=[e_o])
    e_id = dve.op(lambda h: h.tensor_copy(ident[:], tmpf[:, 0, 0:128]), deps=[e_idf])
    e_mk = dve.op(lambda h: h.tensor_copy(maskb[:], trif[:]), deps=[e_tri])
    e_ob = dve.op(lambda h: h.memset(onesb[:], 1.0))
    e_wps = dve.op(lambda h: h.tensor_tensor(
        wps, wps, psb.rearrange("p (g n) -> p g n", g=4).unsqueeze(2).to_broadcast([128, 4, 2, 256]),
        ALU.mult), deps=[e_const, e_misc])

    STAT = {"n": 0}

    def stat_cols(n):
        c = STAT["n"]
        STAT["n"] += n
        assert STAT["n"] <= 512
        return c

    def rstd_from(cols, n, width, deps):
        c = stat_cols(2)
        if n > 1:
            e = dve.op(lambda h: h.tensor_reduce(stats[:, c:c + 1], stats[:, cols:cols + n], AX.X, ALU.add),
                       deps=deps)
            src = stats[:, c:c + 1]
        else:
            e = None
            src = stats[:, cols:cols + 1]
        d2 = [e] if e is not None else list(deps)
        e1 = act.op(lambda h: h.activation(stats[:, c + 1:c + 2], src, AF.Sqrt, bias=EPS, scale=1.0 / width),
                    deps=d2)
        e2 = dve.op(lambda h: h.reciprocal(stats[:, c + 1:c + 2], stats[:, c + 1:c + 2]), deps=[e1])
        return stats[:, c + 1:c + 2], e2

    tpv2 = pb[:, 4:6, :].rearrange("p a n -> p (a n)").bitcast(BF16).rearrange("p (k t) -> p k t", k=KC)
    tpvs = [tpv, tpv2]
    TPB = [(6, 7), (4, 5)]
    tp_state = {"n": 0}

    def transpose_to(src_bf, gi, dstT, t0, deps):
        k = tp_state["n"] % 2
        tp_state["n"] += 1
        b0, b1 = TPB[k]
        tv = tpvs[k]
        d = list(deps) + [bank_free[b0], bank_free[b1], e_id]
        e = None
        for kc in range(KC):
            e = pe.op(lambda h, kc=kc: h.transpose(tv[:, kc, :], src_bf[:, kc * 128:(kc + 1) * 128], ident[:]),
                      deps=d, signal=(kc == KC - 1))
            d = []
        ev = dve.op(lambda h: h.tensor_tensor(
            dstT[:, :, t0:t0 + 128], tv, gc[:, gi, :].unsqueeze(2).to_broadcast([128, KC, 128]), ALU.mult),
            deps=[e, e_misc])
        bank_free[b0] = ev
        bank_free[b1] = ev
        return ev

    xs_free = [None, None]
    xs_n = {"n": 0}

    def junk(width):
        k = xs_n["n"] % 2
        return xsb[:, k, 0:width], xs_free[k]

    def norm_T(src, gi, dstT, t0, deps):
        k = xs_n["n"] % 2
        xs_n["n"] += 1
        xs = xsb[:, k, :]
        c = stat_cols(1)
        e_sq = act.op(lambda h: h.activation(xs, src, AF.Square, accum_out=stats[:, c:c + 1]),
                      deps=list(deps) + [xs_free[k], e_st0])
        r, e_r = rstd_from(c, 1, float(D), [e_sq])
        e_xs = act.op(lambda h: h.activation(xs, src, AF.Copy, scale=r), deps=[e_r])
        ev = transpose_to(xs, gi, dstT, t0, [e_xs])
        xs_free[k] = ev
        return ev, e_xs

    def dbg_exit(name, view, dep):
        if debug and debug[0] == name:
            e_d = sp.dma(dbg, view, deps=[dep], slot=SL_OUT)
            sp.wait(e_d)
            fw.finish()
            return True
        return False

    def hnT_all(kc, tile):
        if tile < 8:
            return hnT_prev[:, kc, tile * 128:(tile + 1) * 128]
        return hnT_own[:, kc, (tile - 8) * 128:(tile - 7) * 128]

    rot = {"n": 0}

    def next_bank4():
        b_ = rot["n"] % 4
        rot["n"] += 1
        return b_

    wu = []
    for blk in range(2):
        wu.append(wload(wview(w_in, 0, D, blk * 512, 512), KC, 512))

    def pooled_unit(cc, half):
        g = cc // 2
        b_ = next_bank4()
        d = [e_u_box[0], e_const, bank_free[b_]]
        e = None
        for i4 in range(4):
            i = half * 4 + i4
            kd = 0 if i == 0 else 1
            pe.op(lambda h: h.matmul(
                bank(b_)[:, i4 * 128:(i4 + 1) * 128], u_bf[:, i + 1, cc * 128:(cc + 1) * 128],
                poolA[:, g * 3 + kd, :], start=True, stop=False), deps=d, signal=False)
            d = []
            e = pe.op(lambda h: h.matmul(
                bank(b_)[:, i4 * 128:(i4 + 1) * 128], u_bf[:, i, cc * 128:(cc + 1) * 128],
                poolA[:, g * 3 + 2, :], start=False, stop=True), signal=(i4 == 3))
        ev_ = act.op(lambda h: h.activation(
            pooledT[:, cc, half * 512:(half + 1) * 512], bank(b_), AF.Copy), deps=[e])
        bank_free[b_] = ev_
        return ev_

    units = [(cc, half) for cc in range(8) for half in range(2)]
    xt_free = [None, None]
    e_hn = {}
    e_u_box = [None]
    pending = None
    e_pT = None
    order = [7] + list(range(8, 16)) + list(range(0, 7))
    for idx, ti in enumerate(order):
        src_d = xp if ti < 8 else xo
        r0 = (ti % 8) * 128
        b_ = idx % 2
        e_ld = sp.dma(xt[b_], src_d[r0:r0 + 128, :], deps=[xt_free[b_]], slot=SL_X0 + b_)
        dstT = hnT_prev if ti < 8 else hnT_own
        ev, e_xs = norm_T(xt[b_], 0, dstT, r0, [e_ld])
        xt_free[b_] = e_xs
        e_hn[ti] = ev
        if pending is not None:
            pending()
            pending = None
        if ti >= 7:
            def pending(ti=ti, ev=ev):
                global_e = None
                t9 = ti - 7
                for blk in range(2):
                    wv_, e_wu, wi = wu[blk]
                    bk = next_bank4()
                    d = [e_wu, ev, bank_free[bk]]
                    e = None
                    for kc in range(KC):
                        e = pe.op(lambda h, kc=kc: h.matmul(bank(bk), hnT_all(kc, ti), wv_[:, kc, :],
                                                            start=(kc == 0), stop=(kc == KC - 1)),
                                  deps=d, signal=(kc == KC - 1))
                        d = []
                    eu_ = act.op(lambda h: h.activation(
                        u_bf[:, t9, blk * 512:(blk + 1) * 512], bank(bk), AF.Copy), deps=[e, e_psb, e_wps])
                    bank_free[bk] = eu_
                    e_u_box[0] = eu_
                    if ti == 15:
                        wrelease(wi, e)
        else:
            for _ in range(2):
                if units:
                    e_pT = pooled_unit(*units.pop(0))
    if pending is not None:
        pending()
    while units:
        e_pT = pooled_unit(*units.pop(0))
    e_hnT = e_hn[6]
    e_hn_own = e_hn[15]

    if dbg_exit("hnT", X[:, 16384:32768].bitcast(F32), e_hnT):
        return nc

    ssq_pool = stat_cols(NT)
    yp = pb[:, 4:6, :].rearrange("p a n -> p (a n)")
    e_yp = None
    for i in range(NT):
        d = [e_pT, e_wps, bank_free[4], bank_free[5]]
        for g in range(4):
            for c2 in range(2):
                e = pe.op(lambda h, i=i, g=g, c2=c2: h.matmul(
                    yp[:, g * 256:(g + 1) * 256], pooledT[:, 2 * g + c2, i * 128:(i + 1) * 128], wps[:, g, c2, :],
                    start=(c2 == 0), stop=(c2 == 1)), deps=d, signal=(g == 3 and c2 == 1))
                d = []
        e1 = act.op(lambda h, i=i: h.activation(yb[:, i, 0:1024], yp, AF.Copy), deps=[e])
        jk, jd = junk(1024)
        e_yp = act.op(lambda h, i=i, jk=jk: h.activation(jk, yp, AF.Square,
                                                         accum_out=stats[:, ssq_pool + i:ssq_pool + i + 1]),
                      deps=[jd])
        bank_free[4] = e_yp
        bank_free[5] = e_yp


    wf, e_wf, wfi = wload(wview(w_in, 0, D, 4096, 8), KC, 8)
    d = [e_wf, e_hnT, bank_free[0]]
    for t in range(16):
        for kc in range(KC):
            e = pe.op(lambda h, t=t, kc=kc: h.matmul(bank(0)[:, t * 8:(t + 1) * 8], hnT_all(kc, t), wf[:, kc, :],
                                                     start=(kc == 0), stop=(kc == KC - 1)),
                      deps=d, signal=(t == 15 and kc == KC - 1))
            d = []
    wrelease(wfi, e)
    e_z = dve.op(lambda h: h.tensor_tensor(
        zf.rearrange("p (t h) -> p t h", t=16), bank(0)[:, 0:128].rearrange("p (t h) -> p t h", t=16),
        bfb[:].unsqueeze(1).to_broadcast([128, 16, NH]), ALU.add), deps=[e, e_misc, e_fs])
    e_e = act.op(lambda h: h.activation(zf, zf, AF.Exp, scale=-1.0), deps=[e_z])
    e_l = act.op(lambda h: h.activation(zf, zf, AF.Ln, bias=1.0), deps=[e_e])
    a3_, a3_dep = junk(384)
    a3_k = xs_n["n"] % 2
    a3 = a3_.rearrange("p (k n) -> p k n", k=3)
    rres = tmpf[:, 1, 0:128]
    e_a = dve.op(lambda h: h.tensor_copy(a3[:, 0, :], zf), deps=[e_l, a3_dep])
    e_a = dve.op(lambda h: h.tensor_tensor(rres, zf, a3[:, 0, :], ALU.subtract), deps=[e_a])
    e_a = dve.op(lambda h: h.tensor_copy(a3[:, 1, :], rres), deps=[e_a])
    e_a = dve.op(lambda h: h.tensor_tensor(rres, rres, a3[:, 1, :], ALU.subtract), deps=[e_a])
    e_a = dve.op(lambda h: h.tensor_copy(a3[:, 2, :], rres), deps=[e_a])
    d = [e_a, e_mk, e_ob, bank_free[1]]
    for k3 in range(3):
        e_w = pe.op(lambda h, k3=k3: h.matmul(bank(1)[:, 0:128], maskb[:], a3[:, k3, :], start=(k3 == 0), stop=(k3 == 2)),
                    deps=d)
        d = []
    for k3 in range(3):
        e_tt = pe.op(lambda h, k3=k3: h.matmul(bank(1)[:, 128:256], onesb[:], a3[:, k3, :], start=(k3 == 0), stop=(k3 == 2)))
    xs_free[a3_k] = e_tt
    e_tot = dve.op(lambda h: h.tensor_copy(tott, bank(1)[:, 128:256]), deps=[e_tt])
    e_off = e_fs
    tot3 = tott.rearrange("p (t h) -> p t h", t=16)
    off3 = offt.rearrange("p (t h) -> p t h", t=16)
    G3 = Gt.rearrange("p (t h) -> p t h", t=16)
    for t in range(1, 16):
        e_off = dve.op(lambda h, t=t: h.tensor_tensor(off3[:, t, :], off3[:, t - 1, :], tot3[:, t - 1, :], ALU.add),
                       deps=[e_off, e_tot])
    e_G = dve.op(lambda h: h.tensor_tensor(Gt, bank(1)[:, 0:128], offt, ALU.add), deps=[e_w, e_off])
    bank_free[0] = e_z
    bank_free[1] = e_G
    e_bias = None
    for hh in range(NH):
        for i in range(NT):
            e_bias = dve.op(lambda h, hh=hh, i=i: h.tensor_scalar(
                biasT[:, hh, i, :], G3[:, :, hh], off3[:, 8 + i, hh:hh + 1], None, ALU.subtract),
                deps=[e_G])

    if dbg_exit("G", GA[:, 0:1024], e_bias):
        return nc
    if dbg_exit("G2", fst[:].rearrange("p a n -> p (a n)"), e_bias):
        return nc

    if dbg_exit("yb", H[:, 0:16384].bitcast(F32), e_yp):
        return nc
    if debug and debug[0] == "poolall":
        e1_ = sp.dma(dbg[:, 0:1024], GA[:, 1024:2048], deps=[e_yp], slot=SL_OUT)
        e2_ = sp.dma(dbg[:, 1024:5120], hnX[:, :].bitcast(F32), deps=[e_yp], slot=SL_OUT)
        e3_ = sp.dma(dbg[:, 5120:8192], H[:, 16384:22528].bitcast(F32), deps=[e_yp], slot=SL_OUT)
        sp.wait(e3_)
        fw.finish()
        return nc

    ssq_attn = stat_cols(NT * NH)
    e_vone = dve.op(lambda h: h.tensor_copy(
        Vaug[:, :, :, 128:129], kval[:].unsqueeze(2).unsqueeze(3).to_broadcast([128, 16, 2, 1])),
        deps=[e_misc, e_pT, e_yp])
    srot = {"n": 0}
    s_free = [None] * 2
    pt_free = [None] * 2
    o_free = [None, None]
    S_BANKS = (2, 3)
    O_SETS = ((4, 5), (6, 7))
    grot = {"n": 0}
    prot2 = {"n": 0}

    def next_bank2():
        b_ = prot2["n"] % 2
        prot2["n"] += 1
        return b_

    e_att_last = None
    for hp in range(4):
        wq, e_wq, wqi = wload(wview(w_in, 0, D, 1024 + hp * 256, 256), KC, 256)
        for hh in range(2):
            for tb in range(2):
                b_ = next_bank2()
                d = [e_wq, e_hn_own, bank_free[b_]]
                for kc in range(KC):
                    e = pe.op(lambda h, kc=kc, hh=hh, tb=tb, b_=b_: h.matmul(
                        bank(b_), wq[:, kc, hh * 128:(hh + 1) * 128], hnT_own[:, kc, tb * 512:(tb + 1) * 512],
                        start=(kc == 0), stop=(kc == KC - 1)), deps=d, signal=(kc == KC - 1))
                    d = []
                ee = act.op(lambda h, hh=hh, tb=tb, b_=b_: h.activation(
                    QT[:, hh, tb * 512:(tb + 1) * 512], bank(b_), AF.Copy), deps=[e, e_att_last, e_yp])
                bank_free[b_] = ee
        wrelease(wqi, e)
        e_q = ee
        wk, e_wk, wki = wload(wview(w_in, 0, D, 2048 + hp * 256, 256), KC, 256)
        for hh in range(2):
            for tb in range(4):
                b_ = next_bank2()
                d = [e_wk, e_hnT, bank_free[b_]]
                src = hnT_prev if tb < 2 else hnT_own
                for kc in range(KC):
                    e = pe.op(lambda h, kc=kc, hh=hh, tb=tb, b_=b_, src=src: h.matmul(
                        bank(b_), wk[:, kc, hh * 128:(hh + 1) * 128], src[:, kc, (tb % 2) * 512:(tb % 2 + 1) * 512],
                        start=(kc == 0), stop=(kc == KC - 1)), deps=d, signal=(kc == KC - 1))
                    d = []
                ee = dve.op(lambda h, hh=hh, tb=tb, b_=b_: h.tensor_copy(
                    KT[:, hh, tb * 512:(tb + 1) * 512], bank(b_)), deps=[e, e_att_last, e_yp])
                bank_free[b_] = ee
        wrelease(wki, e)
        e_k = ee
        wv, e_wv, wvi = wload(wview(w_in, 0, D, 3072 + hp * 256, 256), KC, 256)
        for tile in range(16):
            b_ = next_bank2()
            d = [e_wv, e_hnT, bank_free[b_]]
            for kc in range(KC):
                e = pe.op(lambda h, kc=kc, tile=tile, b_=b_: h.matmul(
                    bank(b_)[:, 0:256], hnT_all(kc, tile), wv[:, kc, :],
                    start=(kc == 0), stop=(kc == KC - 1)), deps=d, signal=(kc == KC - 1))
                d = []
            ee = dve.op(lambda h, tile=tile, b_=b_: h.tensor_scalar(
                Vaug[:, tile, :, 0:128], bank(b_)[:, 0:256].rearrange("p (h d) -> p h d", h=2),
                kval[:, tile:tile + 1], None, ALU.mult), deps=[e, e_att_last, e_vone, e_yp])
            bank_free[b_] = ee
        wrelease(wvi, e)
        e_v = ee
        for hh in range(2):
            hd = 2 * hp + hh
            for g in range(4):
                i0_, i1_ = 2 * g, 2 * g + 1
                oset = grot["n"] % 2
                grot["n"] += 1
                ob0, ob1 = O_SETS[oset]
                oacc0 = bank(ob0)[:, 0:129]
                oacc1 = bank(ob1)[:, 0:129]
                items = [(j, True) for j in range(8 + i0_ + 1)] + [(8 + i1_, False)]
                nit = len(items)

                def emit_s(it):
                    j, full = items[it]
                    sl = srot["n"] % 2
                    srot["n"] += 1
                    w_ = 256 if full else 128
                    q0 = i0_ * 128 if full else i1_ * 128
                    st = bank(S_BANKS[sl])[:, 0:w_]
                    es = pe.op(lambda h: h.matmul(st, KT[:, hh, j * 128:(j + 1) * 128],
                                                  QT[:, hh, q0:q0 + w_], start=True, stop=True),
                               deps=[e_q, e_k, s_free[sl], bank_free[S_BANKS[sl]]])
                    pt = ptb[:, sl, 0:w_]
                    ee_ = act.op(lambda h: h.activation(pt, st, AF.Exp, scale=SCALE, bias=biasT[:, hd, i0_, j:j + 1]),
                                 deps=[es, pt_free[sl], e_bias])
                    s_free[sl] = ee_
                    if (full and j == 8 + i0_) or (not full):
                        ee_ = dve.op(lambda h: h.tensor_tensor(pt[:, 0:128], pt[:, 0:128], maskb[:], ALU.mult),
                                     deps=[ee_, e_mk])
                    return (sl, pt, ee_)

                pend = [emit_s(0)]
                epv = None
                for it in range(nit):
                    if it + 1 < nit:
                        pend.append(emit_s(it + 1))
                    j, full = items[it]
                    sl, pt, ee_ = pend.pop(0)
                    first = [ee_, e_v, o_free[oset], bank_free[ob0], bank_free[ob1]] if it == 0 else [ee_]
                    if full:
                        pe.op(lambda h: h.matmul(oacc0, pt[:, 0:128], Vaug[:, j, hh, 0:129],
                                                 start=(j == 0), stop=(j == 8 + i0_)), deps=first, signal=False)
                        epv = pe.op(lambda h: h.matmul(oacc1, pt[:, 128:256], Vaug[:, j, hh, 0:129],
                                                       start=(j == 0), stop=False))
                    else:
                        epv = pe.op(lambda h: h.matmul(oacc1, pt[:, 0:128], Vaug[:, j, hh, 0:129],
                                                       start=False, stop=True), deps=first)
                    pt_free[sl] = epv
                for (i, oacc) in ((i0_, oacc0), (i1_, oacc1)):
                    c = stat_cols(1)
                    e_r = dve.op(lambda h: h.reciprocal(stats[:, c:c + 1], oacc[:, 128:129]), deps=[epv])
                    e_y = act.op(lambda h: h.activation(
                        yb[:, i, 1024 + hd * 128:1024 + (hd + 1) * 128], oacc[:, 0:128], AF.Copy,
                        scale=stats[:, c:c + 1]), deps=[e_r])
                    jk, jd = junk(128)
                    e_y2 = act.op(lambda h: h.activation(
                        jk, oacc[:, 0:128], AF.Square, scale=stats[:, c:c + 1],
                        accum_out=stats[:, ssq_attn + i * NH + hd:ssq_attn + i * NH + hd + 1]), deps=[jd])
                o_free[oset] = e_y2
                e_att_last = e_y2
    for b_ in (2, 3, 4, 5, 6, 7):
        bank_free[b_] = e_att_last

    e_yT = None
    for i in range(NT):
        rp, e_rp = rstd_from(ssq_pool + i, 1, 1024.0, [e_yp])
        ra, e_ra = rstd_from(ssq_attn + i * NH, NH, 1024.0, [e_att_last])
        e1 = dve.op(lambda h, i=i, rp=rp: h.tensor_scalar(yb[:, i, 0:1024], yb[:, i, 0:1024], rp, None, ALU.mult),
                    deps=[e_rp, e_att_last])
        e2 = dve.op(lambda h, i=i, ra=ra: h.tensor_scalar(yb[:, i, 1024:2048], yb[:, i, 1024:2048], ra, None, ALU.mult),
                    deps=[e_ra])
        e_yT = transpose_to(yb[:, i, :], 3, yT[i // 4], (i % 4) * 128, [e2, bank_free[6], bank_free[7]])

    if debug and debug[0] == "yT":
        e_d = sp.dma(dbg, X[:, 0:16384].bitcast(F32), deps=[e_yT], slot=SL_OUT)
        sp.wait(e_d)
        fw.finish()
        return nc

    setrot = {"n": 0}

    def next_set():
        s_ = setrot["n"] % 2
        setrot["n"] += 1
        return [4 * s_ + k for k in range(4)]

    gb_free = e_att_last
    e_blk_prev = e_yT
    out_evs = []
    h1_free = [None] * 4
    for tb in range(2):
        yTb = yT[tb]
        tok0 = tb * 512
        e_gb = sp.dma(gbuf[:], gb_in[0], deps=[gb_free], slot=SL_G)
        ssq1 = stat_cols(16)
        e_ev = [None] * 4
        for cb in range(4):
            w, e_w_, wi = wload(wview(w_out, 0, D, cb * 512, 512), KC, 512)
            bs = next_set()
            for t in range(4):
                b_ = bs[t]
                d = [e_w_, e_yT, bank_free[b_]]
                for kc in range(KC):
                    e = pe.op(lambda h, kc=kc, t=t, b_=b_, w=w: h.matmul(
                        bank(b_), yTb[:, kc, t * 128:(t + 1) * 128], w[:, kc, :],
                        start=(kc == 0), stop=(kc == KC - 1)), deps=d, signal=(kc == KC - 1))
                    d = []
                e_c = dve.op(lambda h, t=t, cb=cb, b_=b_: h.tensor_copy(h1[:, t, cb * 512:(cb + 1) * 512], bank(b_)),
                             deps=[e, h1_free[t]])
                jk, jd = junk(512)
                e_s = act.op(lambda h, t=t, cb=cb, b_=b_, jk=jk: h.activation(
                    jk, bank(b_), AF.Square, accum_out=stats[:, ssq1 + t * 4 + cb:ssq1 + t * 4 + cb + 1]),
                    deps=[e, e_c, jd])
                bank_free[b_] = [e_c, e_s]
                e_ev[t] = [e_c, e_s]
            wrelease(wi, e)
        e_h1 = [None] * 4
        e_hn2 = None
        for t in range(4):
            b2 = t % 2
            e_ld = sp.dma(xt[b2], xo[tok0 + t * 128:tok0 + (t + 1) * 128, :], deps=[xt_free[b2], e_blk_prev],
                          slot=SL_X0 + b2)
            r, e_r = rstd_from(ssq1 + t * 4, 4, float(D), e_ev[t])
            ea = dve.op(lambda h, t=t, r=r: h.scalar_tensor_tensor(h1[:, t, :], h1[:, t, :], r, gbuf[:], ALU.mult, ALU.mult),
                        deps=[e_r, e_gb])
            eb = dve.op(lambda h, t=t, b2=b2: h.tensor_tensor(h1[:, t, :], h1[:, t, :], xt[b2], ALU.add), deps=[ea, e_ld])
            xt_free[b2] = eb
            e_h1[t] = eb
            e_hn2, _ = norm_T(h1[:, t, :], 1, hn2T, t * 128, [eb])
        gb_free = e_h1[3]

        if debug and debug[0] == "h1" and tb == 0:
            e_d = sp.dma(dbg, X[:, 16384:32768].bitcast(F32), deps=[e_hn2], slot=SL_OUT)
            sp.wait(e_d)
            fw.finish()
            return nc

        e_gb = sp.dma(gbuf[:], gb_in[1], deps=[gb_free], slot=SL_G)
        e_hid = None
        tr = 0
        tmp_free = [None, None]
        for fb in range(16):
            w, e_w_, wi = wload(wview(w_up, 0, D, fb * 512, 512), KC, 512)
            bs = next_set()
            for fc in range(4):
                b_ = bs[fc]
                d = [e_w_, e_hn2, bank_free[b_]]
                for kc in range(KC):
                    e = pe.op(lambda h, kc=kc, fc=fc, b_=b_, w=w: h.matmul(
                        bank(b_), w[:, kc, fc * 128:(fc + 1) * 128], hn2T[:, kc, :],
                        start=(kc == 0), stop=(kc == KC - 1)), deps=d, signal=(kc == KC - 1))
                    d = []
                tb2 = tr % 2
                tr += 1
                e_r = act.op(lambda h, b_=b_, tb2=tb2: h.activation(tmpf[:, tb2, :], bank(b_), AF.Relu),
                             deps=[e, tmp_free[tb2]])
                e_hid = dve.op(lambda h, fb=fb, fc=fc, tb2=tb2: h.tensor_tensor(
                    hidden[:, fb * 4 + fc, :], tmpf[:, tb2, :], tmpf[:, tb2, :], ALU.mult),
                    deps=[e_r, e_blk_prev, e_h1[3]])
                tmp_free[tb2] = e_hid
                bank_free[b_] = e_r
            wrelease(wi, e)

        ssq2 = stat_cols(16)
        e_ev = [[] for _ in range(4)]
        e_m = None
        for q in range(4):
            bs = next_set()
            for kg in range(4):
                w, e_w_, wi = wload(wview(w_down, kg * 2048, 2048, q * 512, 512), KC, 512)
                for t in range(4):
                    b_ = bs[t]
                    d = [e_w_, e_hid] + ([bank_free[b_]] if kg == 0 else [])
                    for kc in range(KC):
                        e = pe.op(lambda h, kc=kc, kg=kg, t=t, b_=b_, w=w: h.matmul(
                            bank(b_), hidden[:, kg * 16 + kc, t * 128:(t + 1) * 128], w[:, kc, :],
                            start=(kg == 0 and kc == 0), stop=(kg == 3 and kc == KC - 1)),
                            deps=d, signal=(kc == KC - 1))
                        d = []
                    if kg == 3:
                        mdst = (m_lo[tb] if q < 2 else m_hi)[:, t, (q % 2) * 512:(q % 2 + 1) * 512]
                        e_c = dve.op(lambda h, mdst=mdst, b_=b_: h.tensor_copy(mdst, bank(b_)),
                                     deps=[e, e_hn2])
                        jk, jd = junk(512)
                        e_s = act.op(lambda h, t=t, q=q, b_=b_, jk=jk: h.activation(
                            jk, bank(b_), AF.Square,
                            accum_out=stats[:, ssq2 + t * 4 + q:ssq2 + t * 4 + q + 1]),
                            deps=[e, e_c, jd])
                        bank_free[b_] = [e_c, e_s]
                        e_ev[t] += [e_c, e_s]
                        e_m = e_c
                wrelease(wi, e)
        e_wdown_done = e
        e_h2 = [None] * 4
        for t in range(4):
            r, e_r = rstd_from(ssq2 + t * 4, 4, float(D), e_ev[t])
            ea = dve.op(lambda h, t=t, r=r: h.scalar_tensor_tensor(
                m_lo[tb][:, t, :], m_lo[tb][:, t, :], r, gbuf[:, 0:1024], ALU.mult, ALU.mult), deps=[e_r, e_gb])
            eb = dve.op(lambda h, t=t, r=r: h.scalar_tensor_tensor(
                m_hi[:, t, :], m_hi[:, t, :], r, gbuf[:, 1024:2048], ALU.mult, ALU.mult), deps=[e_r])
            ec = dve.op(lambda h, t=t: h.tensor_tensor(h1[:, t, 0:1024], h1[:, t, 0:1024], m_lo[tb][:, t, :], ALU.add),
                        deps=[ea])
            ed = dve.op(lambda h, t=t: h.tensor_tensor(h1[:, t, 1024:2048], h1[:, t, 1024:2048], m_hi[:, t, :], ALU.add),
                        deps=[eb])
            e_h2[t] = ed
        gb_free = e_h2[3]
        e_gb = sp.dma(gbuf[:], gb_in[2], deps=[gb_free], slot=SL_G)
        e_hn3 = None
        for t in range(4):
            e_hn3, _ = norm_T(h1[:, t, :], 2, hn2T, t * 128, [e_h2[3]])

        e_pl = sp.dma(pin, p_in[tok0:tok0 + 512, :].rearrange("(t p) n -> p t n", p=128),
                      deps=[e_wdown_done], slot=SL_P)
        e_pb = dve.op(lambda h: h.tensor_copy(pbf, pin), deps=[e_pl])
        d = [e_pb, bank_free[6], bank_free[7], e_hn3]
        for t in range(4):
            for c2 in range(2):
                e = pe.op(lambda h, t=t, c2=c2: h.transpose(
                    tpv[:, t * 2 + c2, :], pbf[:, t, c2 * 128:(c2 + 1) * 128], ident[:]),
                    deps=d, signal=(t == 3 and c2 == 1))
                d = []
        e_pT = dve.op(lambda h: h.tensor_copy(
            pT.rearrange("p k (t n) -> p t k n", t=4), tpv[:, 0:8, :].rearrange("p (t k) n -> p t k n", t=4)),
            deps=[e])
        bank_free[6] = e_pT
        bank_free[7] = e_pT
        wpl_free = [None, None]
        ssq3 = stat_cols(16)
        e_ev = [[] for _ in range(4)]
        prot = 0
        last_e = None
        for cb in range(4):
            wpl = wplb[:, cb % 2, :, :]
            e_wpl = pool.dma(wpl, w_ple[:, cb * 512:(cb + 1) * 512].rearrange("(k p) n -> p k n", p=128),
                             deps=[wpl_free[cb % 2], e_yp], slot=SL_PL0 + cb % 2)
            w, e_w_, wi = wload(wview(w_gate, 0, D, cb * 512, 512), KC, 512)
            for t in range(4):
                bg, be = 2 * (prot % 4), 2 * (prot % 4) + 1
                prot += 1
                d = [e_w_, e_hn3, bank_free[bg], bank_free[be], e_wpl, e_pT]
                for kc in range(KC):
                    e = pe.op(lambda h, kc=kc, t=t, bg=bg, w=w: h.matmul(
                        bank(bg), hn2T[:, kc, t * 128:(t + 1) * 128], w[:, kc, :],
                        start=(kc == 0), stop=(kc == KC - 1)), deps=d, signal=False)
                    d = []
                for c2 in range(2):
                    e = pe.op(lambda h, c2=c2, t=t, be=be, wpl=wpl: h.matmul(
                        bank(be), pT[:, c2, t * 128:(t + 1) * 128], wpl[:, c2, :],
                        start=(c2 == 0), stop=(c2 == 1)), signal=(c2 == 1))
                tb2 = tr % 2
                tr += 1
                e_sg = act.op(lambda h, bg=bg, tb2=tb2: h.activation(tmpf[:, tb2, :], bank(bg), AF.Sigmoid),
                              deps=[e, tmp_free[tb2]])
                e_ge = dve.op(lambda h, t=t, cb=cb, be=be, tb2=tb2: h.tensor_tensor(
                    ge[:, t, cb * 512:(cb + 1) * 512], tmpf[:, tb2, :], bank(be), ALU.mult),
                    deps=[e_sg, e_wdown_done])
                e_q3 = dve.op(lambda h, t=t, cb=cb, tb2=tb2: h.scalar_tensor_tensor(
                    tmpf[:, tb2, :], ge[:, t, cb * 512:(cb + 1) * 512], 1.0, ge[:, t, cb * 512:(cb + 1) * 512],
                    ALU.mult, ALU.mult, accum_out=stats[:, ssq3 + t * 4 + cb:ssq3 + t * 4 + cb + 1]),
                    deps=[e_ge])
                tmp_free[tb2] = e_q3
                bank_free[bg] = e_sg
                bank_free[be] = e_ge
                e_ev[t] = [e_q3]
                last_e = e
            wrelease(wi, last_e)
            wpl_free[cb % 2] = last_e
        for t in range(4):
            r, e_r = rstd_from(ssq3 + t * 4, 4, float(D), e_ev[t])
            ea = dve.op(lambda h, t=t, r=r: h.scalar_tensor_tensor(ge[:, t, :], ge[:, t, :], r, gbuf[:], ALU.mult, ALU.mult),
                        deps=[e_r, e_gb])
            eb = dve.op(lambda h, t=t: h.tensor_tensor(ge[:, t, :], ge[:, t, :], h1[:, t, :], ALU.add), deps=[ea])
            e_o_ = sp.dma(out[tok0 + t * 128:tok0 + (t + 1) * 128, :], ge[:, t, :], deps=[eb], slot=SL_OUT)
            out_evs.append(e_o_)
            e_blk_prev = eb
        e_blk_prev = [e_blk_prev, out_evs[-1]]
        gb_free = eb

    sp.wait(out_evs[-1])
    fw.finish()
    return nc


def _pool_mats(first_half):
    A = np.zeros((128, 4, 3, 128), np.float32)
    s_ = np.arange(128)[:, None]
    t_ = np.arange(128)[None, :]
    for g, w in enumerate(WINDOWS):
        for kind in range(3):
            dist = (t_ + 128 - s_) if kind == 2 else (t_ - s_)
            inwin = (dist >= 0) & (dist < w)
            if kind == 0 and first_half:
                cnt = np.minimum(t_ + 1, w).astype(np.float32)
            else:
                cnt = np.full((1, 128), float(w), np.float32)
            M = np.where(inwin, 1.0 / cnt, 0.0).astype(np.float32)
            if kind != 2:
                M = M - np.eye(128, dtype=np.float32)
            A[:, g, kind, :] = M
    return np.ascontiguousarray(A.reshape(128, 12 * 128))


def _make_in_maps(inp):
    f = lambda a: np.ascontiguousarray(np.asarray(a, dtype=np.float32))
    x = f(inp["x"])
    p = f(inp["p"])[0]
    gcs = [f(inp["g_pre_mix"])[0], f(inp["g_pre_mlp"])[0], f(inp["g_ple_gate"])[0],
           np.concatenate([f(inp["g_pool_out"])[0], f(inp["g_attn_out"])[0]])]
    gc = np.stack([g.reshape(KC, 128).T for g in gcs], axis=1).reshape(128, 4 * KC)
    gb = np.stack([np.broadcast_to(f(inp[k])[0], (128, D)) for k in ("g_post_mix", "g_post_mlp", "g_post_ple")])
    common = {
        "w_in": f(inp["w_in"])[0], "w_out": f(inp["w_out"])[0], "w_up": f(inp["w_up"])[0],
        "w_down": f(inp["w_down"])[0], "w_gate": f(inp["w_gate"])[0], "w_ple": f(inp["w_ple"])[0],
        "w_pool": f(inp["w_pool"])[0],
        "gc": np.ascontiguousarray(gc), "gb": np.ascontiguousarray(gb),
        "psb": np.ascontiguousarray(np.broadcast_to(f(inp["pool_scale"])[0], (128, 1024))),
        "bfb": np.ascontiguousarray(np.broadcast_to(f(inp["b_f"])[0], (128, NH))),
    }
    pa = [_pool_mats(False), _pool_mats(True)]
    maps = []
    for c in range(NC):
        b, hf = c // 2, c % 2
        m = dict(common)
        m["xo"] = np.ascontiguousarray(x[b, hf * TOK:(hf + 1) * TOK])
        m["xp"] = np.ascontiguousarray(x[b, 0:TOK]) if hf == 1 else np.zeros((TOK, D), np.float32)
        m["p"] = np.ascontiguousarray(p[b, hf * TOK:(hf + 1) * TOK])
        kv = np.ones((128, 16), np.float32)
        if hf == 0:
            kv[:, 0:8] = 0.0
        m["kval"] = kv
        m["poolA"] = pa[1 if hf == 0 else 0]
        maps.append(m)
    return maps


def kernel(**inputs):
    maps = _make_in_maps(inputs)
    nc = build_nc()
    res = run_bass_kernel_spmd(nc, maps, core_ids=list(range(NC)))
    outp = np.zeros((B, S, D), np.float32)
    for c in range(NC):
        b, hf = c // 2, c % 2
        outp[b, hf * TOK:(hf + 1) * TOK] = np.asarray(res.results[c]["out"], dtype=np.float32)
    return outp
```
